# Optimizing a Trainium2 kernel written in Bass

```python
import math
import jax
import jax.numpy as jnp
from jax import lax
import numpy as np

D_MODEL = 2048
BATCH = 2
SEQ = 4096
DEPTH = 4
DEC_BATCH = 8
DEC_SEQ = 4096
PAST_LEN = 128

N_MIXERS = 3
N_GDN = (DEPTH + 2) // 3
N_SWA = (DEPTH + 1) // 3
N_FNET = DEPTH // 3

GDN_DK = 128
GDN_DV = 128
GDN_HK = D_MODEL // 128
GDN_HV = 2 * GDN_HK
GDN_QK_DIM = GDN_HK * GDN_DK
GDN_V_DIM = GDN_HV * GDN_DV
GDN_CONV_DIM = 2 * GDN_QK_DIM + GDN_V_DIM
GDN_IN_DIM = GDN_CONV_DIM + GDN_V_DIM + 4 * GDN_HV
GDN_CONV_W = 5
GDN_CHUNK = 64

SWA_DH = 128
SWA_HQ = D_MODEL // SWA_DH
SWA_HKV = 4
SWA_GROUP = SWA_HQ // SWA_HKV
SWA_IN_DIM = (SWA_HQ + 2 * SWA_HKV) * SWA_DH
WINDOW = 128
SWA_BLOCK = 128
REL_BUCKETS = 32
REL_MAX_DIST = 128

FNET_GROUPS = 4

FFN_HIDDEN = -(-8 * D_MODEL // (3 * 256)) * 256

RMS_EPS = 1e-6

kernel_name = 'hybrid_bidir_gdn_swa_fnet_encoder'


def _rmsnorm(x, g):
    xf = x.astype(jnp.float32)
    y = xf * lax.rsqrt(jnp.mean(xf * xf, axis=-1, keepdims=True) + RMS_EPS)
    return (y * g.astype(jnp.float32)).astype(x.dtype)


def _l2norm(x):
    xf = x.astype(jnp.float32)
    return xf * lax.rsqrt(jnp.sum(xf * xf, axis=-1, keepdims=True) + 1e-6)


def _centred_depthwise_conv(x, w):
    c = x.shape[-1]
    pad = GDN_CONV_W // 2
    return lax.conv_general_dilated(
        x, w[:, None, :].astype(x.dtype), window_strides=(1,), padding=[(pad, pad)],
        dimension_numbers=('NWC', 'WIO', 'NWC'), feature_group_count=c)


def _chunk_gated_delta_rule(q, k, v, g, beta):
    B, H, T, DK = k.shape
    DV = v.shape[-1]
    C = GDN_CHUNK
    n = T // C
    q, k, v = (t.reshape(B, H, n, C, t.shape[-1]) for t in (q, k, v))
    beta = beta.reshape(B, H, n, C, 1)
    g = jnp.cumsum(g.reshape(B, H, n, C), axis=-1)
    idx = jnp.arange(C)
    causal = idx[:, None] >= idx[None, :]
    strict = idx[:, None] > idx[None, :]
    decay = jnp.exp(jnp.where(causal, g[..., :, None] - g[..., None, :], -jnp.inf))
    k_beta = k * beta
    lower = jnp.where(strict, jnp.einsum('bhnik,bhnjk->bhnij', k_beta, k) * decay, 0.0)
    rhs = jnp.concatenate([v * beta, k_beta * jnp.exp(g)[..., None]], axis=-1)
    sol = lax.linalg.triangular_solve(lower + jnp.eye(C, dtype=jnp.float32), rhs,
                                      left_side=True, lower=True, unit_diagonal=True)
    u, w = sol[..., :DV], sol[..., DV:]
    qk = jnp.where(causal, jnp.einsum('bhnik,bhnjk->bhnij', q, k) * decay, 0.0)
    g_last = g[..., -1:]
    k_to_end = k * jnp.exp(g_last - g)[..., None]
    q_decay = q * jnp.exp(g)[..., None]
    xs = tuple(jnp.moveaxis(t, 2, 0) for t in (q_decay, k_to_end, u, w, qk, g_last))

    def step(S, inp):
        q_c, k_c, u_c, w_c, qk_c, gl_c = inp
        v_new = u_c - jnp.einsum('bhck,bhkv->bhcv', w_c, S)
        o_c = jnp.einsum('bhck,bhkv->bhcv', q_c, S) + jnp.einsum('bhij,bhjv->bhiv', qk_c, v_new)
        S = S * jnp.exp(gl_c)[..., None] + jnp.einsum('bhck,bhcv->bhkv', k_c, v_new)
        return S, o_c

    S0 = jnp.zeros((B, H, DK, DV), jnp.float32)
    _, o = lax.scan(step, S0, xs)
    return jnp.moveaxis(o, 0, 2).reshape(B, H, T, DV)


def _gated_deltanet(h, w_in, conv_w, a_log, dt_bias, norm_g, w_out):
    B, T, _ = h.shape
    proj = h @ w_in
    qkv, z, ba = jnp.split(proj, [GDN_CONV_DIM, GDN_CONV_DIM + GDN_V_DIM], axis=-1)
    qkv = jax.nn.silu(_centred_depthwise_conv(qkv, conv_w))
    q, k, v = jnp.split(qkv, [GDN_QK_DIM, 2 * GDN_QK_DIM], axis=-1)
    rep = GDN_HV // GDN_HK
    q = jnp.repeat(_l2norm(q.reshape(B, T, GDN_HK, GDN_DK)) * GDN_DK ** -0.5, rep, axis=2)
    k = jnp.repeat(_l2norm(k.reshape(B, T, GDN_HK, GDN_DK)), rep, axis=2)
    v = v.reshape(B, T, GDN_HV, GDN_DV).astype(jnp.float32)
    ba = ba.astype(jnp.float32).reshape(B, T, 2, 2, GDN_HV)
    beta = jax.nn.sigmoid(ba[:, :, :, 0])
    g = -jnp.exp(a_log.astype(jnp.float32)) * jax.nn.softplus(ba[:, :, :, 1] + dt_bias.astype(jnp.float32))
    q, k, v = (jnp.transpose(t, (0, 2, 1, 3)) for t in (q, k, v))
    beta = jnp.transpose(beta, (2, 0, 3, 1))
    g = jnp.transpose(g, (2, 0, 3, 1))
    o_fwd = _chunk_gated_delta_rule(q, k, v, g[0], beta[0])
    o_bwd = jnp.flip(_chunk_gated_delta_rule(jnp.flip(q, 2), jnp.flip(k, 2), jnp.flip(v, 2),
                                             jnp.flip(g[1], 2), jnp.flip(beta[1], 2)), 2)
    o = jnp.transpose(o_fwd + o_bwd, (0, 2, 1, 3))
    o = _rmsnorm(o, norm_g) * jax.nn.silu(z.reshape(B, T, GDN_HV, GDN_DV).astype(jnp.float32))
    return o.reshape(B, T, GDN_V_DIM).astype(h.dtype) @ w_out


def _band_rel_bias(rel_bias):
    qi = jnp.arange(SWA_BLOCK)[:, None]
    kj = jnp.arange(3 * SWA_BLOCK)[None, :]
    rel = kj - SWA_BLOCK - qi
    half = REL_BUCKETS // 2
    max_exact = half // 2
    n = jnp.abs(rel)
    large = max_exact + (jnp.log(jnp.maximum(n, 1).astype(jnp.float32) / max_exact)
                         / math.log(REL_MAX_DIST / max_exact) * (half - max_exact)).astype(jnp.int32)
    large = jnp.minimum(large, half - 1)
    bucket = jnp.where(rel > 0, half, 0) + jnp.where(n < max_exact, n, large)
    bias = rel_bias[bucket].astype(jnp.float32)
    bias = jnp.transpose(bias, (2, 0, 1)).reshape(SWA_HKV, SWA_GROUP, SWA_BLOCK, 3 * SWA_BLOCK)
    return bias, rel


def _swa_attention(h, w_in, sink, w_out, rel_bias):
    B, T, _ = h.shape
    nb = T // SWA_BLOCK
    qkv = h @ w_in
    q, k, v = jnp.split(qkv, [SWA_HQ * SWA_DH, (SWA_HQ + SWA_HKV) * SWA_DH], axis=-1)
    q = q.reshape(B, nb, SWA_BLOCK, SWA_HKV, SWA_GROUP, SWA_DH).transpose(1, 0, 2, 3, 4, 5)
    pad = ((0, 0), (SWA_BLOCK, SWA_BLOCK), (0, 0), (0, 0))
    k = jnp.pad(k.reshape(B, T, SWA_HKV, SWA_DH), pad)
    v = jnp.pad(v.reshape(B, T, SWA_HKV, SWA_DH), pad)
    bias, rel = _band_rel_bias(rel_bias)
    in_window = jnp.abs(rel) <= WINDOW
    sink_l = sink.astype(jnp.float32).reshape(1, SWA_HKV, SWA_GROUP, 1, 1)
    scale = SWA_DH ** -0.5
    kj = jnp.arange(3 * SWA_BLOCK)

    def block(args):
        qb, b = args
        start = b * SWA_BLOCK
        kb = lax.dynamic_slice_in_dim(k, start, 3 * SWA_BLOCK, axis=1)
        vb = lax.dynamic_slice_in_dim(v, start, 3 * SWA_BLOCK, axis=1)
        s = jnp.einsum('bqhgd,bkhd->bhgqk', qb, kb, preferred_element_type=jnp.float32) * scale + bias
        kpos = start - SWA_BLOCK + kj
        valid = in_window & ((kpos >= 0) & (kpos < T))[None, :]
        s = jnp.where(valid, s, -jnp.inf)
        m = jnp.maximum(jnp.max(s, axis=-1, keepdims=True), sink_l)
        p = jnp.exp(s - m)
        denom = jnp.sum(p, axis=-1, keepdims=True) + jnp.exp(sink_l - m)
        p = (p / denom).astype(vb.dtype)
        return jnp.einsum('bhgqk,bkhd->bqhgd', p, vb)

    o = lax.map(block, (q, jnp.arange(nb)))
    o = o.transpose(1, 0, 2, 3, 4, 5).reshape(B, T, SWA_HQ * SWA_DH)
    return o @ w_out


def _fourier_mix(h, w_out):
    B, T, D = h.shape
    hg = h.astype(jnp.float32).reshape(B, T, FNET_GROUPS, D // FNET_GROUPS)
    y = jnp.real(jnp.fft.fft2(hg, axes=(1, 3), norm='ortho'))
    return y.reshape(B, T, D).astype(h.dtype) @ w_out


def _swiglu(h, w_gate_up, w_down):
    gate, up = jnp.split(h @ w_gate_up, 2, axis=-1)
    return (jax.nn.silu(gate) * up) @ w_down


def _trunk(x, pre_mix_g, post_mix_g, pre_ffn_g, post_ffn_g, gdn_w_in, gdn_conv_w, gdn_a_log,
           gdn_dt_bias, gdn_norm_g, gdn_w_out, swa_w_in, swa_sink, swa_w_out, rel_bias,
           fnet_w_out, ffn_w_gate_up, ffn_w_down):
    for i in range(DEPTH):
        kind, j = i % N_MIXERS, i // N_MIXERS
        h = _rmsnorm(x, pre_mix_g[i])
        if kind == 0:
            m = _gated_deltanet(h, gdn_w_in[j], gdn_conv_w[j], gdn_a_log[j], gdn_dt_bias[j],
                                gdn_norm_g[j], gdn_w_out[j])
        elif kind == 1:
            m = _swa_attention(h, swa_w_in[j], swa_sink[j], swa_w_out[j], rel_bias)
        else:
            m = _fourier_mix(h, fnet_w_out[j])
        x = x + _rmsnorm(m, post_mix_g[i])
        h = _rmsnorm(x, pre_ffn_g[i])
        x = x + _rmsnorm(_swiglu(h, ffn_w_gate_up[i], ffn_w_down[i]), post_ffn_g[i])
    return x


def setup_inputs(seed: int = 0) -> dict:
    key = jax.random.key(seed)
    ks = jax.random.split(key, 20)
    f32 = jnp.float32

    def nrm(k, shape, scale):
        return jax.random.normal(k, shape, f32) * scale

    dt = jnp.exp(jax.random.uniform(ks[9], (N_GDN, 2, GDN_HV), f32, math.log(1e-3), math.log(1e-1)))
    return {
        'x_prompt': nrm(ks[0], (BATCH, SEQ, D_MODEL), 1.0),
        'x_sample': nrm(ks[1], (DEC_BATCH, DEC_SEQ, D_MODEL), 1.0),
        'pre_mix_g': 1.0 + nrm(ks[2], (DEPTH, D_MODEL), 0.02),
        'post_mix_g': 1.0 + nrm(ks[3], (DEPTH, D_MODEL), 0.02),
        'pre_ffn_g': 1.0 + nrm(ks[4], (DEPTH, D_MODEL), 0.02),
        'post_ffn_g': 1.0 + nrm(ks[5], (DEPTH, D_MODEL), 0.02),
        'gdn_w_in': nrm(ks[6], (N_GDN, D_MODEL, GDN_IN_DIM), D_MODEL ** -0.5),
        'gdn_conv_w': nrm(ks[7], (N_GDN, GDN_CONV_W, GDN_CONV_DIM), GDN_CONV_W ** -0.5),
        'gdn_a_log': jnp.log(jax.random.uniform(ks[8], (N_GDN, 2, GDN_HV), f32, 1.0, 16.0)),
        'gdn_dt_bias': dt + jnp.log(-jnp.expm1(-dt)),
        'gdn_norm_g': 1.0 + nrm(ks[10], (N_GDN, GDN_DV), 0.02),
        'gdn_w_out': nrm(ks[11], (N_GDN, GDN_V_DIM, D_MODEL), GDN_V_DIM ** -0.5),
        'swa_w_in': nrm(ks[12], (N_SWA, D_MODEL, SWA_IN_DIM), D_MODEL ** -0.5),
        'swa_sink': nrm(ks[13], (N_SWA, SWA_HQ), 0.5),
        'swa_w_out': nrm(ks[14], (N_SWA, SWA_HQ * SWA_DH, D_MODEL), (SWA_HQ * SWA_DH) ** -0.5),
        'rel_bias': nrm(ks[15], (REL_BUCKETS, SWA_HQ), 0.5),
        'fnet_w_out': nrm(ks[16], (N_FNET, D_MODEL, D_MODEL), D_MODEL ** -0.5),
        'ffn_w_gate_up': nrm(ks[17], (DEPTH, D_MODEL, 2 * FFN_HIDDEN), D_MODEL ** -0.5),
        'ffn_w_down': nrm(ks[18], (DEPTH, FFN_HIDDEN, D_MODEL), FFN_HIDDEN ** -0.5),
    }


def reference(x_prompt, x_sample, pre_mix_g, post_mix_g, pre_ffn_g, post_ffn_g, gdn_w_in,
              gdn_conv_w, gdn_a_log, gdn_dt_bias, gdn_norm_g, gdn_w_out, swa_w_in, swa_sink,
              swa_w_out, rel_bias, fnet_w_out, ffn_w_gate_up, ffn_w_down):
    y_prompt = _trunk(x_prompt, pre_mix_g, post_mix_g, pre_ffn_g, post_ffn_g, gdn_w_in, gdn_conv_w,
                      gdn_a_log, gdn_dt_bias, gdn_norm_g, gdn_w_out, swa_w_in, swa_sink, swa_w_out,
                      rel_bias, fnet_w_out, ffn_w_gate_up, ffn_w_down)
    y_sample = _trunk(x_sample, pre_mix_g, post_mix_g, pre_ffn_g, post_ffn_g, gdn_w_in, gdn_conv_w,
                      gdn_a_log, gdn_dt_bias, gdn_norm_g, gdn_w_out, swa_w_in, swa_sink, swa_w_out,
                      rel_bias, fnet_w_out, ffn_w_gate_up, ffn_w_down)
    return (y_prompt, y_sample)
```

```python
import math
import os
import numpy as np
import ml_dtypes
import concourse.bass as bass
import concourse.mybir as mybir
from concourse.bass_utils import run_bass_kernel_spmd

F32 = mybir.dt.float32
BF16 = mybir.dt.bfloat16
ALU = mybir.AluOpType
AF = mybir.ActivationFunctionType
AX = mybir.AxisListType
P = 128
EPOCH = int(os.environ.get("EPOCH", "30000"))
DMA_K = 8
EMBED_WAIT = bool(int(os.environ.get('EMBED_WAIT', '1')))

D_MODEL = 2048
KC_D = D_MODEL // P
FFN_H = 5632
RMS_EPS = 1e-6


class Tracker:
    def __init__(self, fw, name, is_dma):
        self.fw = fw
        self.name = name
        self.is_dma = is_dma
        self.count = 0
        self.sems = []

    def sem_for(self, cnt):
        if self.is_dma:
            i = cnt - 1
            return self.sems[i % DMA_K], 16 * (i // DMA_K + 1)
        e = (cnt - 1) // EPOCH
        return self._epoch_sem(e), (cnt - 1) % EPOCH + 1

    def _epoch_sem(self, e):
        while len(self.sems) <= e:
            self.sems.append(self.fw.nc.alloc_semaphore(f"{self.name}_e{len(self.sems)}"))
        return self.sems[e]


class Issuer:
    def __init__(self, fw, name):
        self.fw = fw
        self.name = name
        self.stream = []
        self.comp = Tracker(fw, name + "_c", False)
        self.dma = Tracker(fw, name + "_d", True)
        self.seen = {}
        self.seen_dma = {}


class Buf:
    __slots__ = ("lw", "rd", "name")

    def __init__(self, name=""):
        self.lw = None
        self.rd = []
        self.name = name


class FW:
    def __init__(self, nc):
        self.nc = nc
        self.iss = {n: Issuer(self, n) for n in ("pe", "dve", "act", "pool", "sp")}
        for n in ("sp", "pool", "act"):
            t = self.iss[n].dma
            t.sems = [nc.alloc_semaphore(f"{n}_dma{i}") for i in range(DMA_K)]
        self.n_ops = 0

    def _need(self, I, deps, tr, cnt):
        if tr.is_dma:
            s = I.seen_dma.setdefault(tr, set())
            if cnt in s:
                return
            mx = max(s) if s else 0
            if cnt <= mx - DMA_K:
                return
            key = (tr, cnt)
            deps[key] = True
        else:
            if I.seen.get(tr, 0) >= cnt:
                return
            key = (tr, 0)
            deps[key] = max(deps.get(key, 0), cnt)

    def op(self, iname, fn, reads=(), writes=(), dma=False):
        I = self.iss[iname]
        deps = {}
        for b in reads:
            if b.lw is not None:
                self._need(I, deps, *b.lw)
        for b in writes:
            if b.lw is not None:
                self._need(I, deps, *b.lw)
            for r in b.rd:
                self._need(I, deps, *r)
        tr = I.dma if dma else I.comp
        my_cnt = tr.count + 1
        if dma and my_cnt > DMA_K:
            self._need(I, deps, tr, my_cnt - DMA_K)
        waits = []
        for (t, c), v in deps.items():
            cnt = c if t.is_dma else v
            if iname == "pe" and t is I.comp:
                continue
            sem, val = t.sem_for(cnt)
            waits.append((sem, val))
            if t.is_dma:
                s = I.seen_dma.setdefault(t, set())
                s.add(cnt)
                if len(s) > 4 * DMA_K:
                    mx = max(s)
                    I.seen_dma[t] = {x for x in s if x > mx - 2 * DMA_K}
            else:
                I.seen[t] = cnt
        emb = None
        if EMBED_WAIT and waits and not dma:
            emb = waits.pop()
        for (sem, val) in waits:
            I.stream.append(lambda e, sem=sem, val=val: e.wait_ge(sem, val))
        tr.count = my_cnt
        sem, _ = tr.sem_for(my_cnt)
        inc = 16 if dma else 1
        if emb is None:
            I.stream.append(lambda e, fn=fn, sem=sem, inc=inc: fn(e).then_inc(sem, inc))
        else:
            I.stream.append(lambda e, fn=fn, sem=sem, inc=inc, emb=emb: fn(e)._wait_ge(emb[0], emb[1]).then_inc(sem, inc))
        me = (tr, my_cnt)
        for b in writes:
            b.lw = me
            b.rd = []
        for b in reads:
            if b.lw is not me:
                b.rd.append(me)
                if len(b.rd) > 24:
                    b.rd = self._prune(b.rd)
        self.n_ops += 1
        return me

    @staticmethod
    def _prune(rd):
        best = {}
        extra = []
        for (t, c) in rd:
            if t.is_dma:
                extra.append((t, c))
            else:
                best[t] = max(best.get(t, 0), c)
        return [(t, c) for t, c in best.items()] + extra[-2 * DMA_K:]

    def final_wait(self, iname, bufs):
        I = self.iss[iname]
        deps = {}
        for b in bufs:
            if b.lw is not None:
                self._need(I, deps, *b.lw)
        for (t, c), v in deps.items():
            cnt = c if t.is_dma else v
            sem, val = t.sem_for(cnt)
            I.stream.append(lambda e, sem=sem, val=val: e.wait_ge(sem, val))

    def emit(self):
        nc = self.nc
        with nc.Block() as block:
            @block.tensor
            def _(e):
                for f in self.iss["pe"].stream:
                    f(e)

            @block.vector
            def _(e):
                for f in self.iss["dve"].stream:
                    f(e)

            @block.scalar
            def _(e):
                for f in self.iss["act"].stream:
                    f(e)

            @block.gpsimd
            def _(e):
                for f in self.iss["pool"].stream:
                    f(e)

            @block.sync
            def _(e):
                for f in self.iss["sp"].stream:
                    f(e)


class Prog:
    def __init__(self, T, nslot, layers, n_gdn, n_swa, n_fnet, depth):
        self.T = T
        self.nslot = nslot
        self.layers = layers
        self.depth = depth
        nc = bass.Bass("TRN2", target_bir_lowering=False)
        self.nc = nc
        self.fw = FW(nc)
        self.n_gdn, self.n_swa, self.n_fnet = n_gdn, n_swa, n_fnet
        self._ev_rr = 0
        self._st_rr = 0
        self.declare_dram()
        self.alloc_common()

    def dram(self, name, shape, dt, kind="Internal"):
        return self.nc.dram_tensor(name, list(shape), dt, kind=kind).ap()

    def declare_dram(self):
        T, ns = self.T, self.nslot
        d = self.dram
        self.x_in = d("x_in", [ns, T, D_MODEL], F32, "ExternalInput")
        self.y_out = d("y_out", [ns, T, D_MODEL], F32, "ExternalOutput")
        self.y_buf = Buf("y_out")
        dep = self.depth
        self.g_pre_mix = d("pre_mix_g", [dep, D_MODEL], F32, "ExternalInput")
        self.g_post_mix = d("post_mix_g", [dep, D_MODEL], F32, "ExternalInput")
        self.g_pre_ffn = d("pre_ffn_g", [dep, D_MODEL], F32, "ExternalInput")
        self.g_post_ffn = d("post_ffn_g", [dep, D_MODEL], F32, "ExternalInput")
        self.w_gu = d("ffn_w_gate_up", [dep, D_MODEL, 2 * FFN_H], F32, "ExternalInput")
        self.w_dn = d("ffn_w_down", [dep, FFN_H, D_MODEL], F32, "ExternalInput")
        self.wb_gu = d("wb_gu", [dep, 2 * FFN_H // P, P, KC_D, P], BF16)
        self.wb_dn = d("wb_dn", [dep, KC_D, P, FFN_H // P, P], BF16)
        self.wb_buf = Buf("wb")
        self.XT = d("XT", [KC_D, P, T], F32)
        self.XT_buf = Buf("XT")

    def sb(self, name, shape, dt):
        return self.nc.alloc_sbuf_tensor(name, list(shape), dt)

    def alloc_common(self):
        nc = self.nc
        self.ident_f = self.sb("ident_f", [P, P], F32)
        self.ident_b = self.sb("ident_b", [P, P], BF16)
        self.ones_b = self.sb("ones_b", [P, P], BF16)
        self.cb = Buf("consts")
        self.gains = self.sb("gains", [P, 4, self.depth, KC_D], F32)
        self.ps = [nc.alloc_psum_tensor(f"ps{i}", [P, 512], F32) for i in range(8)]
        self.psb = [Buf(f"ps{i}") for i in range(8)]
        TT = 512
        self.TT = TT
        self.xt = self.sb("xt", [P, KC_D, TT], F32)
        self.xtb = Buf("xt")
        self.mt = self.sb("mt", [P, KC_D, TT], F32)
        self.mtb = Buf("mt")
        self.hs = self.sb("hs", [P, 2, KC_D, TT], BF16)
        self.ht = self.hs[:, 0]
        self.htb = Buf("ht")
        self.sq = self.hs[:, 1]
        self.sqb = Buf("sq")
        self.arena = self.sb("arena", [P, 44, TT], BF16)
        self.arb = Buf("arena")
        self.fdummy = self.sb("fdummy", [P, 2], F32)
        self.rstd = self.sb("rstd", [P, TT], F32)
        self.rstdb = Buf("rstd")
        self.tmpf = [self.sb(f"tmpf{i}", [P, TT], F32) for i in range(2)]
        self.tmpfb = [Buf(f"tmpf{i}") for i in range(2)]
        NW = 3
        self.NW = NW
        self.wt = [self.sb(f"wt{i}", [P, 44 * P], BF16) for i in range(NW)]
        self.wtb = [Buf(f"wt{i}") for i in range(NW)]
        self.wt_rr = 0
        mflat = self.mt[:, :, :].rearrange("p c t -> p (c t)")
        self.cvf = [mflat[:, 0:2048], mflat[:, 2048:4096]]
        self.cvfb = [Buf(f"cvf{i}") for i in range(2)]
        self.cvb = [mflat[:, 4096:5120].bitcast(BF16), mflat[:, 5120:6144].bitcast(BF16)]
        self.cvbb = [Buf(f"cvb{i}") for i in range(2)]

    def fence(self, extra=()):
        bufs = [self.xtb, self.mtb, self.htb, self.sqb, self.arb, self.rstdb, self.tmpfb[0], self.tmpfb[1]] + list(extra)
        self.op("dve", lambda e: e.memset(self.fdummy[:, 0:1], 0.0), writes=bufs)

    def conv_fence(self):
        self.op("dve", lambda e: e.memset(self.rstd[:, 0:1], 0.0),
                writes=self.cvfb + self.cvbb + [self.mtb, self.rstdb])

    def op(self, *a, **k):
        return self.fw.op(*a, **k)

    def init_consts(self):
        op = self.op
        ident_f_dram = self.dram("c_ident_f", [P, P], F32, "ExternalInput")
        ident_b_dram = self.dram("c_ident_b", [P, P], BF16, "ExternalInput")
        ones_b_dram = self.dram("c_ones_b", [P, P], BF16, "ExternalInput")
        op("sp", lambda e: e.dma_start(out=self.ident_f[:], in_=ident_f_dram[:, :]), writes=[self.cb], dma=True)
        op("sp", lambda e: e.dma_start(out=self.ident_b[:], in_=ident_b_dram[:, :]), writes=[self.cb], dma=True)
        op("sp", lambda e: e.dma_start(out=self.ones_b[:], in_=ones_b_dram[:, :]), writes=[self.cb], dma=True)
        for k, g in enumerate((self.g_pre_mix, self.g_post_mix, self.g_pre_ffn, self.g_post_ffn)):
            for l in range(self.depth):
                src = g[l].rearrange("(c p) -> p c", p=P)
                op("sp", lambda e, k=k, l=l, src=src: e.dma_start(
                    out=self.gains[:, k, l, :], in_=src, allow_slow_non_contiguous=True),
                   writes=[self.cb], dma=True)

    def convert_weight(self, src2d, dst_tiled, K, N):
        op = self.op
        NB = 2048
        i = 0
        for kc in range(K // P):
            for n0 in range(0, N, NB):
                nb = min(NB, N - n0)
                s = i % 2
                i += 1
                f, fb, b, bb = self.cvf[s], self.cvfb[s], self.cvb[s], self.cvbb[s]
                op("sp", lambda e, f=f, kc=kc, n0=n0, nb=nb: e.dma_start(
                    out=f[:, 0:nb], in_=src2d[kc * P:(kc + 1) * P, n0:n0 + nb]),
                   writes=[fb], dma=True)
                eng = "pool" if (i % 2) else "dve"
                op(eng, lambda e, f=f, b=b, nb=nb: e.tensor_copy(out=b[:, 0:nb], in_=f[:, 0:nb]),
                   reads=[fb], writes=[bb])
                dst = dst_tiled[n0 // P:(n0 + nb) // P, :, kc, :].rearrange("n p j -> p n j")
                op("pool", lambda e, b=b, nb=nb, dst=dst: e.dma_start(
                    out=dst, in_=b[:, 0:nb].rearrange("p (n j) -> p n j", j=P)),
                   reads=[bb], writes=[self.wb_buf], dma=True)

    def evac_engine(self):
        self._ev_rr += 1
        return "dve" if self._ev_rr % 2 else "act"

    def load_wtile(self, wtiled_chunk, KC):
        s = self.wt_rr % self.NW
        self.wt_rr += 1
        w, wb = self.wt[s], self.wtb[s]
        view = w[:, 0:KC * P].rearrange("p (k j) -> p k j", j=P)
        self.op("sp", lambda e, view=view, src=wtiled_chunk: e.dma_start(out=view, in_=src),
                reads=[self.wb_buf], writes=[wb], dma=True)
        return view, wb

    def dense_fm(self, src, srcb, KC, wtiled, nchunks, consume, tt, ps_ids=(0, 1)):
        op = self.op
        pref = 2
        tiles = {}
        for n in range(min(pref, nchunks)):
            tiles[n] = self.load_wtile(wtiled[n], KC)
        for n in range(nchunks):
            if n + pref < nchunks:
                tiles[n + pref] = self.load_wtile(wtiled[n + pref], KC)
            w, wb = tiles.pop(n)
            pid = ps_ids[n % len(ps_ids)]
            ps, psb = self.ps[pid], self.psb[pid]
            for k in range(KC):
                op("pe", lambda e, ps=ps, w=w, k=k: e.matmul(
                    ps[:, 0:tt], lhsT=w[:, k, :], rhs=src[:, k, :], start=(k == 0), stop=(k == KC - 1)),
                   reads=[wb, srcb], writes=[psb])
            consume(n, ps, psb)

    def rms_stats(self, x, xb, KC, tt, inv_dim, ps_id=7):
        op = self.op
        for c in range(KC):
            op("act", lambda e, c=c: e.activation(out=self.sq[:, c, 0:tt], in_=x[:, c, :], func=AF.Square),
               reads=[xb], writes=[self.sqb])
        ps, psb = self.ps[ps_id], self.psb[ps_id]
        for c in range(KC):
            op("pe", lambda e, c=c: e.matmul(ps[:, 0:tt], lhsT=self.ones_b[:], rhs=self.sq[:, c, 0:tt],
                                             start=(c == 0), stop=(c == KC - 1)),
               reads=[self.sqb, self.cb], writes=[psb])
        op("dve", lambda e: e.tensor_scalar(out=self.rstd[:, 0:tt], in0=ps[:, 0:tt], scalar1=inv_dim,
                                            scalar2=RMS_EPS, op0=ALU.mult, op1=ALU.add),
           reads=[psb], writes=[self.rstdb])
        op("act", lambda e: e.activation(out=self.rstd[:, 0:tt], in_=self.rstd[:, 0:tt], func=AF.Sqrt),
           reads=[self.rstdb], writes=[self.rstdb])
        op("dve", lambda e: e.reciprocal(out=self.rstd[:, 0:tt], in_=self.rstd[:, 0:tt]),
           reads=[self.rstdb], writes=[self.rstdb])

    def norm_to_h(self, x, xb, gk, layer, tt):
        self.rms_stats(x, xb, KC_D, tt, 1.0 / D_MODEL)
        for c in range(KC_D):
            self.op("dve", lambda e, c=c: e.scalar_tensor_tensor(
                out=self.ht[:, c, 0:tt], in0=x[:, c, :], scalar=self.gains[:, gk, layer, c:c + 1],
                in1=self.rstd[:, 0:tt], op0=ALU.mult, op1=ALU.mult),
                    reads=[xb, self.rstdb, self.cb], writes=[self.htb])

    def norm_residual(self, m, mb, gk, layer, tt):
        self.rms_stats(m, mb, KC_D, tt, 1.0 / D_MODEL)
        for c in range(KC_D):
            eng = "dve"
            self.op(eng, lambda e, c=c: e.scalar_tensor_tensor(
                out=m[:, c, :], in0=m[:, c, :], scalar=self.gains[:, gk, layer, c:c + 1],
                in1=self.rstd[:, 0:tt], op0=ALU.mult, op1=ALU.mult),
                    reads=[mb, self.rstdb, self.cb], writes=[mb])
            self.op("pool", lambda e, c=c: e.tensor_tensor(
                out=self.xt[:, c, 0:tt], in0=self.xt[:, c, 0:tt], in1=m[:, c, :], op=ALU.add),
                    reads=[mb, self.xtb], writes=[self.xtb])

    def phase_convert_ffn(self):
        for l in range(self.depth):
            self.convert_weight(self.w_gu[l], self.wb_gu[l], D_MODEL, 2 * FFN_H)
            self.convert_weight(self.w_dn[l], self.wb_dn[l], FFN_H, D_MODEL)
        self.conv_fence()

    def phase_ingest(self, slot):
        op = self.op
        T, TT = self.T, self.TT
        for t0 in range(0, T, TT):
            for q in range(TT // P):
                stage = self.mt[:, :, :].rearrange("p c t -> p (c t)")[:, q * D_MODEL:(q + 1) * D_MODEL]
                op("sp", lambda e, stage=stage, t0=t0, q=q: e.dma_start(
                    out=stage, in_=self.x_in[slot, t0 + q * P:t0 + (q + 1) * P, :]),
                   writes=[self.mtb], dma=True)
            for q in range(TT // P):
                stage = self.mt[:, :, :].rearrange("p c t -> p (c t)")[:, q * D_MODEL:(q + 1) * D_MODEL]
                for c4 in range(KC_D // 4):
                    pid = 2 + (c4 % 2)
                    ps, psb = self.ps[pid], self.psb[pid]
                    for j in range(4):
                        c = c4 * 4 + j
                        op("pe", lambda e, ps=ps, j=j, c=c, stage=stage: e.transpose(
                            ps[:, j * P:(j + 1) * P], stage[:, c * P:(c + 1) * P], self.ident_f[:]),
                           reads=[self.mtb, self.cb], writes=[psb])
                    eng = self.evac_engine()
                    dst = self.xt[:, c4 * 4:(c4 + 1) * 4, q * P:(q + 1) * P]
                    src = ps[:, :].rearrange("p (j t) -> p j t", t=P)
                    if eng == "dve":
                        op("dve", lambda e, dst=dst, src=src: e.tensor_copy(out=dst, in_=src),
                           reads=[psb], writes=[self.xtb])
                    else:
                        op("act", lambda e, dst=dst, src=src: e.activation(out=dst, in_=src, func=AF.Copy),
                           reads=[psb], writes=[self.xtb])
            op("pool", lambda e, t0=t0: e.dma_start(out=self.XT[:, :, t0:t0 + TT].rearrange("c p t -> p c t"),
                                                   in_=self.xt[:, :, :]),
               reads=[self.xtb], writes=[self.XT_buf], dma=True)

    def load_x_tile(self, t0):
        TT = self.TT
        self.op("sp", lambda e: e.dma_start(out=self.xt[:, :, :],
                                            in_=self.XT[:, :, t0:t0 + TT].rearrange("c p t -> p c t")),
                reads=[self.XT_buf], writes=[self.xtb], dma=True)

    def store_x_tile(self, t0):
        TT = self.TT
        self.op("pool", lambda e: e.dma_start(out=self.XT[:, :, t0:t0 + TT].rearrange("c p t -> p c t"),
                                              in_=self.xt[:, :, :]),
                reads=[self.xtb], writes=[self.XT_buf], dma=True)

    def ffn_on_tile(self, layer):
        op = self.op
        TT = self.TT
        self.norm_to_h(self.xt, self.xtb, 2, layer, TT)
        NH = FFN_H // P
        pref = 2
        order = []
        for j in range(NH):
            order.append(("g", j))
            order.append(("u", j))
        tiles = {}

        def wsrc(kind, j):
            n = j if kind == "g" else NH + j
            return self.wb_gu[layer, n]

        for i in range(min(pref, len(order))):
            tiles[i] = self.load_wtile(wsrc(*order[i]), KC_D)
        for i, (kind, j) in enumerate(order):
            if i + pref < len(order):
                tiles[i + pref] = self.load_wtile(wsrc(*order[i + pref]), KC_D)
            w, wb = tiles.pop(i)
            pid = (0 if kind == "g" else 2) + (j % 2)
            ps, psb = self.ps[pid], self.psb[pid]
            for k in range(KC_D):
                op("pe", lambda e, ps=ps, w=w, k=k: e.matmul(
                    ps[:, 0:TT], lhsT=w[:, k, :], rhs=self.ht[:, k, :], start=(k == 0), stop=(k == KC_D - 1)),
                   reads=[wb, self.htb], writes=[psb])
            if kind == "u":
                pg, pgb = self.ps[j % 2], self.psb[j % 2]
                tf, tfb = self.tmpf[j % 2], self.tmpfb[j % 2]
                op("act", lambda e, tf=tf, pg=pg: e.activation(out=tf[:, :], in_=pg[:, 0:TT], func=AF.Silu),
                   reads=[pgb], writes=[tfb])
                op("dve", lambda e, tf=tf, ps=ps, j=j: e.tensor_tensor(
                    out=self.arena[:, j, :], in0=tf[:, :], in1=ps[:, 0:TT], op=ALU.mult),
                   reads=[tfb, psb], writes=[self.arb])

        def consume(n, ps, psb):
            eng = self.evac_engine()
            if eng == "dve":
                op("dve", lambda e: e.tensor_copy(out=self.mt[:, n, :], in_=ps[:, 0:TT]),
                   reads=[psb], writes=[self.mtb])
            else:
                op("act", lambda e: e.activation(out=self.mt[:, n, :], in_=ps[:, 0:TT], func=AF.Copy),
                   reads=[psb], writes=[self.mtb])

        self.dense_fm(self.arena, self.arb, NH, self.wb_dn[layer], KC_D, consume, TT, ps_ids=(4, 5))
        self.norm_residual(self.mt, self.mtb, 3, layer, TT)

    def egress_tile(self, slot, t0):
        op = self.op
        TT = self.TT
        for q in range(TT // P):
            stage = self.mt[:, :, :].rearrange("p c t -> p (c t)")[:, q * D_MODEL:(q + 1) * D_MODEL]
            for c4 in range(KC_D // 4):
                pid = 2 + (c4 % 2)
                ps, psb = self.ps[pid], self.psb[pid]
                for j in range(4):
                    c = c4 * 4 + j
                    op("pe", lambda e, ps=ps, j=j, c=c, q=q: e.transpose(
                        ps[:, j * P:(j + 1) * P], self.xt[:, c, q * P:(q + 1) * P], self.ident_f[:]),
                       reads=[self.xtb, self.cb], writes=[psb])
                eng = self.evac_engine()
                dst = stage[:, c4 * 512:(c4 + 1) * 512]
                if eng == "dve":
                    op("dve", lambda e, dst=dst, ps=ps: e.tensor_copy(out=dst, in_=ps[:, :]),
                       reads=[psb], writes=[self.mtb])
                else:
                    op("act", lambda e, dst=dst, ps=ps: e.activation(out=dst, in_=ps[:, :], func=AF.Copy),
                       reads=[psb], writes=[self.mtb])
            op("pool", lambda e, stage=stage, q=q: e.dma_start(
                out=self.y_out[slot, t0 + q * P:t0 + (q + 1) * P, :], in_=stage),
               reads=[self.mtb], writes=[self.y_buf], dma=True)

    def finish(self):
        self.fw.final_wait("sp", [self.y_buf])
        self.fw.final_wait("pool", [self.y_buf])
        self.fw.emit()


def _copy_evac(self, eng, out, in_, reads, writes, scale=None):
    if eng == "dve":
        if scale is None:
            self.op("dve", lambda e: e.tensor_copy(out=out, in_=in_), reads=reads, writes=writes)
        else:
            self.op("dve", lambda e: e.tensor_scalar(out=out, in0=in_, scalar1=scale, scalar2=None, op0=ALU.mult),
                    reads=reads, writes=writes)
    else:
        if scale is None:
            self.op("act", lambda e: e.activation(out=out, in_=in_, func=AF.Copy), reads=reads, writes=writes)
        else:
            self.op("act", lambda e: e.activation(out=out, in_=in_, func=AF.Copy, scale=scale),
                    reads=reads, writes=writes)


Prog.copy_evac = _copy_evac


def _phase_post(self, layer, src, srcbuf, KC, wtiled_out, slot, last):
    T, TT = self.T, self.TT
    for t0 in range(0, T, TT):
        self.load_x_tile(t0)
        self.op("sp", lambda e, t0=t0: e.dma_start(
            out=self.arena[:, 0:KC, :], in_=src[:, :, t0:t0 + TT].rearrange("c p t -> p c t")),
                reads=[srcbuf], writes=[self.arb], dma=True)

        def consume(n, ps, psb):
            self.copy_evac(self.evac_engine(), self.mt[:, n, :], ps[:, 0:TT], [psb], [self.mtb])

        self.dense_fm(self.arena[:, 0:KC, :], self.arb, KC, wtiled_out, KC_D, consume, TT, ps_ids=(4, 5))
        self.norm_residual(self.mt, self.mtb, 1, layer, TT)
        self.ffn_on_tile(layer)
        if last:
            self.egress_tile(slot, t0)
        else:
            self.store_x_tile(t0)


Prog.phase_post = _phase_post


def _fnet_setup(self):
    d = self.dram
    T = self.T
    self.w_fnet = d("fnet_w_out", [self.n_fnet, D_MODEL, D_MODEL], F32, "ExternalInput")
    self.wb_fnet = d("wb_fnet", [self.n_fnet, KC_D, P, KC_D, P], BF16)
    self.c_cc = d("c_cc", [512, 512], BF16, "ExternalInput")
    self.c_sc = d("c_sc", [512, 512], BF16, "ExternalInput")
    self.c_ct = d("c_ct", [T, T], BF16, "ExternalInput")
    self.c_nst = d("c_nst", [T, T], BF16, "ExternalInput")
    self.ATOK = d("ATOK", [T, D_MODEL], BF16)
    self.BTOK = d("BTOK", [T, D_MODEL], BF16)
    self.ab_buf = Buf("ATOK")
    self.YT = d("YT", [KC_D, P, T], BF16)
    self.yt_buf = Buf("YT")
    self.ccs = self.sb("ccs", [P, 2, 4, 512], BF16)
    self.ccsb = Buf("ccs")


def _fnet_layer(self, layer, j, slot, last):
    op = self.op
    T, TT = self.T, self.TT
    NCH = T // P
    for i, src in enumerate((self.c_cc, self.c_sc)):
        op("sp", lambda e, i=i, src=src: e.dma_start(out=self.ccs[:, i, :, :],
                                                     in_=src.rearrange("(k p) n -> p k n", p=P)),
           writes=[self.ccsb], dma=True)
    stage = self.arena[:, :, :].rearrange("p c t -> p (c t)")
    for t0 in range(0, T, TT):
        self.load_x_tile(t0)
        self.norm_to_h(self.xt, self.xtb, 0, layer, TT)
        for q in range(TT // P):
            for ab in range(2):
                for g in range(4):
                    pid = (g + 4 * ab) % 4
                    ps, psb = self.ps[pid], self.psb[pid]
                    for kc in range(4):
                        op("pe", lambda e, ps=ps, g=g, kc=kc, q=q, ab=ab: e.matmul(
                            ps[:, :], lhsT=self.ht[:, g * 4 + kc, q * P:(q + 1) * P], rhs=self.ccs[:, ab, kc, :],
                            start=(kc == 0), stop=(kc == 3)),
                           reads=[self.htb, self.ccsb], writes=[psb])
                    dst = stage[:, (q * 2 + ab) * D_MODEL + g * 512:(q * 2 + ab) * D_MODEL + (g + 1) * 512]
                    self.copy_evac(self.evac_engine(), dst, ps[:, :], [psb], [self.arb])
                dram = self.ATOK if ab == 0 else self.BTOK
                srcv = stage[:, (q * 2 + ab) * D_MODEL:(q * 2 + ab + 1) * D_MODEL]
                op("pool", lambda e, dram=dram, srcv=srcv, r0=t0 + q * P: e.dma_start(
                    out=dram[r0:r0 + P, :], in_=srcv), reads=[self.arb], writes=[self.ab_buf], dma=True)
    ctv = self.xt[:, :, :].rearrange("p c t -> p (c t)").bitcast(BF16)[:, 0:NCH * 512].rearrange("p (n t) -> p n t", t=512)
    stv = self.mt[:, :, :].rearrange("p c t -> p (c t)").bitcast(BF16)[:, 0:NCH * 512].rearrange("p (n t) -> p n t", t=512)
    scale = 1.0 / math.sqrt(T * 512.0)
    abv = stage[:, 0:4 * NCH * P].rearrange("p (s n j) -> p s n j", s=4, j=P)
    for tp in range(T // 512):
        op("sp", lambda e, tp=tp: e.dma_start(out=ctv, in_=self.c_ct[:, tp * 512:(tp + 1) * 512].rearrange(
            "(n p) t -> p n t", p=P)), writes=[self.xtb], dma=True)
        op("sp", lambda e, tp=tp: e.dma_start(out=stv, in_=self.c_nst[:, tp * 512:(tp + 1) * 512].rearrange(
            "(n p) t -> p n t", p=P)), writes=[self.mtb], dma=True)
        for c in range(KC_D):
            op("sp", lambda e, c=c: e.dma_start(
                out=abv[:, 0, :, :], in_=self.ATOK[:, c * P:(c + 1) * P].rearrange("(n p) j -> p n j", p=P)),
               reads=[self.ab_buf], writes=[self.arb], dma=True)
            op("sp", lambda e, c=c: e.dma_start(
                out=abv[:, 1, :, :], in_=self.BTOK[:, c * P:(c + 1) * P].rearrange("(n p) j -> p n j", p=P)),
               reads=[self.ab_buf], writes=[self.arb], dma=True)
            pid = 4 + (c % 2)
            ps, psb = self.ps[pid], self.psb[pid]
            for n in range(NCH):
                op("pe", lambda e, ps=ps, n=n: e.matmul(ps[:, :], lhsT=abv[:, 0, n, :], rhs=ctv[:, n, :],
                                                        start=(n == 0), stop=False),
                   reads=[self.arb, self.xtb], writes=[psb])
            for n in range(NCH):
                op("pe", lambda e, ps=ps, n=n: e.matmul(ps[:, :], lhsT=abv[:, 1, n, :], rhs=stv[:, n, :],
                                                        start=False, stop=(n == NCH - 1)),
                   reads=[self.arb, self.mtb], writes=[psb])
            self.copy_evac(self.evac_engine(), self.ht[:, c, :], ps[:, :], [psb], [self.htb], scale=scale)
        op("pool", lambda e, tp=tp: e.dma_start(
            out=self.YT[:, :, tp * 512:(tp + 1) * 512].rearrange("c p t -> p c t"), in_=self.ht[:, :, :]),
           reads=[self.htb], writes=[self.yt_buf], dma=True)
    self.phase_post(layer, self.YT, self.yt_buf, KC_D, self.wb_fnet[j], slot, last)


Prog.fnet_setup = _fnet_setup
Prog.fnet_layer = _fnet_layer


def common_consts():
    bf = ml_dtypes.bfloat16
    return {"c_ident_f": np.eye(P, dtype=np.float32), "c_ident_b": np.eye(P).astype(bf),
            "c_ones_b": np.ones((P, P)).astype(bf)}


def fnet_consts(T):
    bf = ml_dtypes.bfloat16
    k = np.arange(512, dtype=np.int64)
    ang = 2.0 * np.pi * ((k[:, None] * k[None, :]) % 512).astype(np.float64) / 512.0
    t = np.arange(T, dtype=np.int64)
    angt = 2.0 * np.pi * ((t[:, None] * t[None, :]) % T).astype(np.float64) / T
    return {"c_cc": np.cos(ang).astype(np.float32).astype(bf), "c_sc": np.sin(ang).astype(np.float32).astype(bf),
            "c_ct": np.cos(angt).astype(np.float32).astype(bf), "c_nst": (-np.sin(angt)).astype(np.float32).astype(bf)}


SWA_NQ, SWA_NKV = 16, 4


def swa_consts():
    import jax
    import jax.numpy as jnp
    with jax.default_device(jax.devices("cpu")[0]):
        rel = jnp.arange(511) - 255
        half, max_exact = 16, 8
        n = jnp.abs(rel)
        large = max_exact + (jnp.log(jnp.maximum(n, 1).astype(jnp.float32) / max_exact)
                             / math.log(128 / max_exact) * (half - max_exact)).astype(jnp.int32)
        large = jnp.minimum(large, half - 1)
        bucket = np.asarray(jnp.where(rel > 0, half, 0) + jnp.where(n < max_exact, n, large))
    oh = np.zeros((33, 512), np.float32)
    for i in range(511):
        oh[bucket[i], i] = 1.0
        if abs(i - 255) > 128:
            oh[32, i] = 1.0
    return {"c_swa_oh": oh}


def _swa_setup(self):
    d = self.dram
    T = self.T
    self.w_swa_in = d("swa_w_in", [self.n_swa, D_MODEL, 3072], F32, "ExternalInput")
    self.w_swa_out = d("swa_w_out", [self.n_swa, D_MODEL, D_MODEL], F32, "ExternalInput")
    self.swa_sink = d("swa_sink", [self.n_swa, SWA_NQ], F32, "ExternalInput")
    self.rel_bias = d("rel_bias", [32, SWA_NQ], F32, "ExternalInput")
    self.c_swa_oh = d("c_swa_oh", [33, 512], F32, "ExternalInput")
    self.wb_swa_in = d("wb_swa_in", [self.n_swa, 24, P, KC_D, P], BF16)
    self.wb_swa_out = d("wb_swa_out", [self.n_swa, KC_D, P, KC_D, P], BF16)
    self.SQT = d("SQT", [SWA_NQ + SWA_NKV, P, T], BF16)
    self.sqt_buf = Buf("SQT")
    self.SVTOK = d("SVTOK", [T, 512], BF16)
    self.svt_buf = Buf("SVTOK")
    self.SOT = d("SOT", [SWA_NQ, P, T], BF16)
    self.sot_buf = Buf("SOT")
    self.FV = d("FV", [SWA_NQ, 512], F32)
    self.fv_buf = Buf("FV")
    self.sinkt = self.sb("sinkt", [P, SWA_NQ], F32)
    self.rbt = self.sb("rbt", [33, SWA_NQ], F32)
    self.oht = self.sb("oht", [33, 512], F32)
    self.fvt = self.sb("fvt", [SWA_NQ, 512], F32)
    self.swab = Buf("swa_small")


def _swa_convert(self):
    for j in range(self.n_swa):
        self.convert_weight(self.w_swa_in[j], self.wb_swa_in[j], D_MODEL, 3072)
        self.convert_weight(self.w_swa_out[j], self.wb_swa_out[j], D_MODEL, D_MODEL)


def _swa_layer(self, layer, j, slot, last):
    op = self.op
    T, TT = self.T, self.TT
    NB = T // P
    op("sp", lambda e: e.dma_start(out=self.rbt[0:32, :], in_=self.rel_bias[:, :]), writes=[self.swab], dma=True)
    op("dve", lambda e: e.memset(self.rbt[32:33, :], -30000.0), writes=[self.swab])
    op("sp", lambda e: e.dma_start(out=self.oht[:, :], in_=self.c_swa_oh[:, :]), writes=[self.swab], dma=True)
    op("sp", lambda e: e.dma_start(out=self.sinkt[:, :], in_=self.swa_sink[j:j + 1, :].partition_broadcast(P)),
       writes=[self.swab], dma=True)
    ps, psb = self.ps[6], self.psb[6]
    op("pe", lambda e: e.matmul(ps[0:SWA_NQ, :], lhsT=self.rbt[:, :], rhs=self.oht[:, :], start=True, stop=True),
       reads=[self.swab], writes=[psb])
    op("dve", lambda e: e.tensor_copy(out=self.fvt[:, :], in_=ps[0:SWA_NQ, :]), reads=[psb], writes=[self.swab])
    op("sp", lambda e: e.dma_start(out=self.FV[:, :], in_=self.fvt[:, :]), reads=[self.swab],
       writes=[self.fv_buf], dma=True)
    mflat_b = self.mt[:, :, :].rearrange("p c t -> p (c t)").bitcast(BF16)
    wv = mflat_b[:, 0:4 * KC_D * P].rearrange("p (n k j) -> p n k j", n=4, k=KC_D)
    op("sp", lambda e: e.dma_start(out=wv, in_=self.wb_swa_in[j, 20:24].rearrange("n p k j -> p n k j")),
       reads=[self.wb_buf], writes=[self.mtb], dma=True)
    vstage = mflat_b[:, 8192:8192 + 4 * 512].rearrange("p (q f) -> p q f", f=512)
    for t0 in range(0, T, TT):
        self.load_x_tile(t0)
        self.norm_to_h(self.xt, self.xtb, 0, layer, TT)

        def consume(n, ps_, psb_):
            self.copy_evac(self.evac_engine(), self.arena[:, n, :], ps_[:, 0:TT], [psb_], [self.arb])

        self.dense_fm(self.ht, self.htb, KC_D, self.wb_swa_in[j], SWA_NQ + SWA_NKV, consume, TT, ps_ids=(0, 1))
        op("pool", lambda e, t0=t0: e.dma_start(
            out=self.SQT[:, :, t0:t0 + TT].rearrange("c p t -> p c t"), in_=self.arena[:, 0:SWA_NQ + SWA_NKV, :]),
           reads=[self.arb], writes=[self.sqt_buf], dma=True)
        for q in range(TT // P):
            pv, pvb = self.ps[2 + q % 2], self.psb[2 + q % 2]
            for kc in range(KC_D):
                op("pe", lambda e, pv=pv, kc=kc, q=q: e.matmul(
                    pv[:, :], lhsT=self.ht[:, kc, q * P:(q + 1) * P], rhs=wv[:, :, kc, :],
                    start=(kc == 0), stop=(kc == KC_D - 1)), reads=[self.htb, self.mtb], writes=[pvb])
            self.copy_evac(self.evac_engine(), vstage[:, q, :], pv[:, :], [pvb], [self.sqb])
        op("pool", lambda e, t0=t0: e.dma_start(
            out=self.SVTOK[t0:t0 + TT, :].rearrange("(q p) f -> p q f", p=P), in_=vstage),
           reads=[self.sqb], writes=[self.svt_buf], dma=True)
    xflat = self.xt[:, :, :].rearrange("p c t -> p (c t)")
    bias = xflat[:, 0:SWA_NQ * 384].rearrange("p (h k) -> p h k", k=384)
    for q in range(P):
        op("sp", lambda e, q=q: e.dma_start(out=bias[q:q + 1, :, :], in_=self.FV[:, 127 - q:127 - q + 384].unsqueeze(0)),
           reads=[self.fv_buf], writes=[self.xtb], dma=True)
    aflat = self.arena[:, :, :].rearrange("p c t -> p (c t)")
    qT = aflat[:, 0:4 * T].rearrange("p (h t) -> p h t", t=T)
    kT = aflat[:, 4 * T:5 * T]
    hflat = self.hs[:, :, :, :].rearrange("p a c t -> p (a c t)")
    vall = hflat[:, 0:T].rearrange("p (n j) -> p n j", j=P)
    vb_ = self.htb
    pT = hflat[:, 8192:8192 + 3 * 512].rearrange("p (k f) -> p k f", f=512)
    pTb = self.sqb
    pn = hflat[:, 12288:12288 + 384]
    ost = hflat[:, 14336:14336 + 512]
    mflat = self.mt[:, :, :].rearrange("p c t -> p (c t)")
    sbuf_s = mflat[:, 0:384]
    pbuf = mflat[:, 512:896]
    st = self.rstd
    scale = 128.0 ** -0.5
    for kh in range(SWA_NKV):
        op("sp", lambda e, kh=kh: e.dma_start(out=qT, in_=self.SQT[kh * 4:(kh + 1) * 4].rearrange("h p t -> p h t")),
           reads=[self.sqt_buf], writes=[self.arb], dma=True)
        op("sp", lambda e, kh=kh: e.dma_start(out=kT, in_=self.SQT[SWA_NQ + kh]),
           reads=[self.sqt_buf], writes=[self.arb], dma=True)
        op("sp", lambda e, kh=kh: e.dma_start(
            out=vall, in_=self.SVTOK[:, kh * P:(kh + 1) * P].rearrange("(n p) j -> p n j", p=P)),
           reads=[self.svt_buf], writes=[vb_], dma=True)
        for b in range(NB):
            lo, hi = max(b - 1, 0), min(b + 1, NB - 1)
            nkb = hi - lo + 1
            nk = nkb * P
            off = (lo - (b - 1)) * P
            tpv = [self.ps[2 + kb // 2][:, :].bitcast(BF16)[:, (kb % 2) * 512:(kb % 2 + 1) * 512] for kb in range(3)]
            tpb = [self.psb[2 + kb // 2] for kb in range(3)]
            for hq in range(4):
                h = kh * 4 + hq
                sp_, spb = self.ps[hq % 2], self.psb[hq % 2]
                op("pe", lambda e, sp_=sp_, hq=hq, b=b, lo=lo, nk=nk: e.matmul(
                    sp_[:, 0:nk], lhsT=qT[:, hq, b * P:(b + 1) * P], rhs=kT[:, lo * P:lo * P + nk],
                    start=True, stop=True), reads=[self.arb], writes=[spb])
                op("dve", lambda e, sp_=sp_, h=h, nk=nk, off=off: e.scalar_tensor_tensor(
                    out=sbuf_s[:, 0:nk], in0=sp_[:, 0:nk], scalar=scale, in1=bias[:, h, off:off + nk],
                    op0=ALU.mult, op1=ALU.add), reads=[spb, self.xtb], writes=[self.mtb])
                op("dve", lambda e, nk=nk: e.tensor_reduce(out=st[:, 0:1], in_=sbuf_s[:, 0:nk], axis=AX.X, op=ALU.max),
                   reads=[self.mtb], writes=[self.rstdb])
                op("dve", lambda e, h=h: e.tensor_scalar(out=st[:, 1:2], in0=st[:, 0:1], scalar1=self.sinkt[:, h:h + 1],
                                                         scalar2=-1.0, op0=ALU.max, op1=ALU.mult),
                   reads=[self.rstdb, self.swab], writes=[self.rstdb])
                op("act", lambda e, nk=nk: e.activation(out=pbuf[:, 0:nk], in_=sbuf_s[:, 0:nk], func=AF.Exp,
                                                        bias=st[:, 1:2], scale=1.0, accum_out=st[:, 2:3]),
                   reads=[self.mtb, self.rstdb], writes=[self.mtb, self.rstdb])
                op("act", lambda e, h=h: e.activation(out=st[:, 3:4], in_=st[:, 1:2], func=AF.Exp,
                                                      bias=self.sinkt[:, h:h + 1], scale=1.0),
                   reads=[self.rstdb, self.swab], writes=[self.rstdb])
                op("dve", lambda e: e.tensor_tensor(out=st[:, 4:5], in0=st[:, 2:3], in1=st[:, 3:4], op=ALU.add),
                   reads=[self.rstdb], writes=[self.rstdb])
                op("dve", lambda e: e.reciprocal(out=st[:, 5:6], in_=st[:, 4:5]),
                   reads=[self.rstdb], writes=[self.rstdb])
                op("dve", lambda e, nk=nk: e.tensor_scalar(out=pn[:, 0:nk], in0=pbuf[:, 0:nk], scalar1=st[:, 5:6],
                                                           scalar2=None, op0=ALU.mult),
                   reads=[self.mtb, self.rstdb], writes=[self.tmpfb[0]])
                for kb in range(nkb):
                    op("pe", lambda e, kb=kb, hq=hq, tpv=tpv: e.transpose(
                        tpv[kb][:, hq * P:(hq + 1) * P], pn[:, kb * P:(kb + 1) * P], self.ident_b[:]),
                       reads=[self.tmpfb[0], self.cb], writes=[tpb[kb]])
            for kb in range(nkb):
                self.copy_evac("dve", pT[:, kb, :], tpv[kb], [tpb[kb]], [pTb])
            ops_, opsb = self.ps[4 + b % 2], self.psb[4 + b % 2]
            for kb in range(nkb):
                op("pe", lambda e, ops_=ops_, kb=kb, lo=lo, nkb=nkb: e.matmul(
                    ops_[:, :], lhsT=vall[:, lo + kb, :], rhs=pT[:, kb, :], start=(kb == 0), stop=(kb == nkb - 1)),
                   reads=[vb_, pTb], writes=[opsb])
            self.copy_evac(self.evac_engine(), ost, ops_[:, :], [opsb], [self.tmpfb[1]])
            op("pool", lambda e, kh=kh, b=b: e.dma_start(
                out=self.SOT[kh * 4:(kh + 1) * 4, :, b * P:(b + 1) * P].rearrange("h p t -> p h t"),
                in_=ost.rearrange("p (h t) -> p h t", t=P)),
               reads=[self.tmpfb[1]], writes=[self.sot_buf], dma=True)
    self.phase_post(layer, self.SOT, self.sot_buf, KC_D, self.wb_swa_out[j], slot, last)


Prog.swa_setup = _swa_setup
Prog.swa_convert = _swa_convert
Prog.swa_layer = _swa_layer


GDN_IN = 12416
GDN_NCHK = 97


def gdn_consts():
    i = np.arange(P)
    low_s = (i[:, None] > i[None, :]).astype(np.float32)
    up_i = (i[None, :] >= i[:, None]).astype(np.float32)
    tri = np.stack([low_s, up_i, low_s.T.copy(), up_i.T.copy(), np.ones((P, P), np.float32)], 0)
    lv = np.zeros((14, P, P), np.float32)
    for l_ in range(7):
        b_ = 1 << l_
        m_ = ((i[:, None] // (2 * b_) == i[None, :] // (2 * b_)) & (i[:, None] % (2 * b_) >= b_)
              & (i[None, :] % (2 * b_) < b_)).astype(np.float32)
        lv[l_] = m_
        lv[7 + l_] = m_.T
    return {"c_tri": np.ascontiguousarray(tri), "c_lvm": lv.astype(ml_dtypes.bfloat16)}


def _gdn_setup(self):
    d = self.dram
    T = self.T
    n = self.n_gdn
    self.w_gdn_in = d("gdn_w_in", [n, D_MODEL, GDN_IN], F32, "ExternalInput")
    self.gdn_conv_w = d("gdn_conv_w", [n, 5, 8192], F32, "ExternalInput")
    self.gdn_a_log = d("gdn_a_log", [n, 2, 32], F32, "ExternalInput")
    self.gdn_dt_bias = d("gdn_dt_bias", [n, 2, 32], F32, "ExternalInput")
    self.gdn_norm_g = d("gdn_norm_g", [n, P], F32, "ExternalInput")
    self.w_gdn_out = d("gdn_w_out", [n, 4096, D_MODEL], F32, "ExternalInput")
    self.c_tri = d("c_tri", [5, P, P], F32, "ExternalInput")
    self.wb_gdn_in = d("wb_gdn_in", [n, GDN_NCHK, P, KC_D, P], BF16)
    self.wb_gdn_out = d("wb_gdn_out", [n, KC_D, P, 32, P], BF16)
    self.RAW = d("RAW", [96, P, T], BF16)
    self.RAWBA = d("RAWBA", [P, T], F32)
    self.raw_buf = Buf("RAW")
    dbg = "ExternalOutput" if os.environ.get("GDN_DEBUG") else "Internal"
    self.GQT = d("GQT", [32, P, T], BF16, dbg)
    self.gqt_buf = Buf("GQT")
    self.GTOK = d("GTOK", [T, 48 * P], BF16, dbg)
    self.gtok_buf = Buf("GTOK")
    self.GOT = d("GOT", [32, P, T], BF16, dbg)
    self.got_buf = Buf("GOT")
    self.c_lvm = d("c_lvm", [14, P, P], BF16, "ExternalInput")
    self.lvm = self.sb("lvm", [P, 14, P], BF16)
    self.tri = self.sb("tri", [P, 5, P], F32)
    self.ntri = self.sb("ntri", [P, 2, P], F32)
    self.cw = self.sb("cw", [P, 5, 64], F32)
    self.gsm = self.sb("gsm", [P, 4], F32)
    self.gcb = Buf("gdn_consts")


def _gdn_convert(self):
    for j in range(self.n_gdn):
        self.convert_weight(self.w_gdn_in[j], self.wb_gdn_in[j], D_MODEL, GDN_IN)
        self.convert_weight(self.w_gdn_out[j], self.wb_gdn_out[j], 4096, D_MODEL)


def _gdn_layer(self, layer, j, slot, last):
    op = self.op
    T, TT = self.T, self.TT
    NCH = T // P
    op("sp", lambda e: e.dma_start(out=self.tri[:, :, :], in_=self.c_tri.rearrange("k p f -> p k f")),
       writes=[self.gcb], dma=True)
    op("sp", lambda e: e.dma_start(out=self.lvm[:, :, :], in_=self.c_lvm.rearrange("k p f -> p k f")),
       writes=[self.gcb], dma=True)
    cwsrc = self.gdn_conv_w[j].rearrange("w (c p) -> (w c) p", p=P)
    cwst = self.tmpf[0][:, 0:384].rearrange("p (k f) -> p k f", f=P)
    cwflat = self.cw[:, :, :].rearrange("p w c -> p (w c)")
    for k_ in range(3):
        rows = min(P, 320 - k_ * P)
        op("sp", lambda e, k_=k_, rows=rows: e.dma_start(out=cwst[0:rows, k_, :], in_=cwsrc[k_ * P:k_ * P + rows, :]),
           writes=[self.tmpfb[0]], dma=True)
        op("pe", lambda e, k_=k_, rows=rows: e.transpose(self.ps[6][:, k_ * P:k_ * P + rows], cwst[0:rows, k_, :],
                                                          self.ident_f[0:rows, 0:rows]),
           reads=[self.tmpfb[0], self.cb], writes=[self.psb[6]])
    op("dve", lambda e: e.tensor_copy(out=cwflat, in_=self.ps[6][:, 0:320]), reads=[self.psb[6]], writes=[self.gcb])
    op("dve", lambda e: e.memset(self.gsm[:, :], 0.0), writes=[self.gcb])
    for dd in range(2):
        r0 = dd * 64 + 32
        op("sp", lambda e, dd=dd, r0=r0: e.dma_start(
            out=self.gsm[r0:r0 + 32, 0:1], in_=self.gdn_a_log[j, dd].rearrange("(h o) -> h o", o=1)),
           writes=[self.gcb], dma=True)
        op("sp", lambda e, dd=dd, r0=r0: e.dma_start(
            out=self.gsm[r0:r0 + 32, 1:2], in_=self.gdn_dt_bias[j, dd].rearrange("(h o) -> h o", o=1)),
           writes=[self.gcb], dma=True)
    op("sp", lambda e: e.dma_start(out=self.gsm[:, 2:3], in_=self.gdn_norm_g[j].rearrange("(h o) -> h o", o=1)),
       writes=[self.gcb], dma=True)
    op("act", lambda e: e.activation(out=self.gsm[:, 0:1], in_=self.gsm[:, 0:1], func=AF.Exp),
       reads=[self.gcb], writes=[self.gcb])
    op("dve", lambda e: e.tensor_scalar(out=self.gsm[:, 0:1], in0=self.gsm[:, 0:1], scalar1=-1.0, scalar2=None,
                                        op0=ALU.mult), reads=[self.gcb], writes=[self.gcb])
    op("dve", lambda e: e.tensor_scalar(out=self.ntri[:, 0, :], in0=self.tri[:, 0, :], scalar1=-1.0, scalar2=None,
                                        op0=ALU.mult), reads=[self.gcb], writes=[self.gcb])
    op("dve", lambda e: e.tensor_scalar(out=self.ntri[:, 1, :], in0=self.tri[:, 2, :], scalar1=-1.0, scalar2=None,
                                        op0=ALU.mult), reads=[self.gcb], writes=[self.gcb])
    for t0 in range(0, T, TT):
        self.load_x_tile(t0)
        self.norm_to_h(self.xt, self.xtb, 0, layer, TT)

        def consume(n, ps_, psb_, t0=t0):
            s = n % 2
            if n < 96:
                st = self.tmpf[s][:, :].bitcast(BF16)[:, 0:TT]
                self.copy_evac(self.evac_engine(), st, ps_[:, 0:TT], [psb_], [self.tmpfb[s]])
                op("pool", lambda e: e.dma_start(out=self.RAW[n, :, t0:t0 + TT], in_=st),
                   reads=[self.tmpfb[s]], writes=[self.raw_buf], dma=True)
            else:
                st = self.tmpf[s][:, 0:TT]
                self.copy_evac(self.evac_engine(), st, ps_[:, 0:TT], [psb_], [self.tmpfb[s]])
                op("pool", lambda e: e.dma_start(out=self.RAWBA[:, t0:t0 + TT], in_=st),
                   reads=[self.tmpfb[s]], writes=[self.raw_buf], dma=True)

        self.dense_fm(self.ht, self.htb, KC_D, self.wb_gdn_in[j], GDN_NCHK, consume, TT, ps_ids=(0, 1))
    aflat = self.arena[:, :, :].rearrange("p c t -> p (c t)")
    xflat = self.xt[:, :, :].rearrange("p c t -> p (c t)")
    mflat = self.mt[:, :, :].rearrange("p c t -> p (c t)")
    hflat = self.hs[:, :, :, :].rearrange("p a c t -> p (a c t)")
    rawp = aflat[:, 0:T + 4]
    acc, sil = xflat[:, 0:T], xflat[:, T:2 * T]
    rst = mflat[:, 0:T]
    sqv, outb = hflat[:, 0:T], hflat[:, T:2 * T]
    tokst = hflat[:, 2 * T:3 * T].rearrange("p (n j) -> p n j", j=P)
    tokb = self.tmpfb[0]
    op("dve", lambda e: e.memset(rawp[:, 0:2], 0.0), writes=[self.arb])
    op("dve", lambda e: e.memset(rawp[:, T + 2:T + 4], 0.0), writes=[self.arb])
    for c in range(64):
        op("sp", lambda e, c=c: e.dma_start(out=rawp[:, 2:T + 2], in_=self.RAW[c]),
           reads=[self.raw_buf], writes=[self.arb], dma=True)
        op("dve", lambda e, c=c: e.tensor_scalar(out=acc, in0=rawp[:, 0:T], scalar1=self.cw[:, 0, c:c + 1], scalar2=None,
                                                 op0=ALU.mult), reads=[self.arb, self.gcb], writes=[self.xtb])
        for k in range(1, 5):
            op("dve", lambda e, c=c, k=k: e.scalar_tensor_tensor(
                out=acc, in0=rawp[:, k:k + T], scalar=self.cw[:, k, c:c + 1], in1=acc, op0=ALU.mult, op1=ALU.add),
               reads=[self.arb, self.gcb, self.xtb], writes=[self.xtb])
        if c < 32:
            op("act", lambda e: e.activation(out=sil, in_=acc, func=AF.Silu), reads=[self.xtb], writes=[self.xtb])
            op("act", lambda e: e.activation(out=sqv, in_=sil, func=AF.Square), reads=[self.xtb], writes=[self.htb])
            for t0 in range(0, T, 512):
                pid = 4 + (t0 // 512) % 2
                ps_, psb_ = self.ps[pid], self.psb[pid]
                op("pe", lambda e, ps_=ps_, t0=t0: e.matmul(ps_[:, :], lhsT=self.ones_b[:], rhs=sqv[:, t0:t0 + 512],
                                                            start=True, stop=True),
                   reads=[self.htb, self.cb], writes=[psb_])
                op("dve", lambda e, ps_=ps_, t0=t0: e.tensor_scalar(out=rst[:, t0:t0 + 512], in0=ps_[:, :], scalar1=1e-6,
                                                                    scalar2=None, op0=ALU.add),
                   reads=[psb_], writes=[self.mtb])
            op("act", lambda e: e.activation(out=rst, in_=rst, func=AF.Sqrt), reads=[self.mtb], writes=[self.mtb])
            op("dve", lambda e: e.reciprocal(out=rst, in_=rst), reads=[self.mtb], writes=[self.mtb])
            sc = (128.0 ** -0.5) if c < 16 else 1.0
            op("dve", lambda e, sc=sc: e.scalar_tensor_tensor(out=outb, in0=sil, scalar=sc, in1=rst, op0=ALU.mult,
                                                              op1=ALU.mult),
               reads=[self.xtb, self.mtb], writes=[self.sqb])
            op("pool", lambda e, c=c: e.dma_start(out=self.GQT[c], in_=outb), reads=[self.sqb],
               writes=[self.gqt_buf], dma=True)
        else:
            op("act", lambda e: e.activation(out=outb, in_=acc, func=AF.Silu), reads=[self.xtb], writes=[self.sqb])
        if c >= 16:
            for n8 in range(0, NCH, 8):
                pid = 2 + (n8 // 8) % 2
                ps_, psb_ = self.ps[pid], self.psb[pid]
                pv = ps_[:, :].bitcast(BF16)
                nn = min(8, NCH - n8)
                for q in range(nn):
                    op("pe", lambda e, pv=pv, q=q, n8=n8: e.transpose(
                        pv[:, q * P:(q + 1) * P], outb[:, (n8 + q) * P:(n8 + q + 1) * P], self.ident_b[:]),
                       reads=[self.sqb, self.cb], writes=[psb_])
                self.copy_evac("dve", tokst[:, n8:n8 + nn, :],
                               pv[:, 0:nn * P].rearrange("p (n j) -> p n j", j=P), [psb_], [tokb])
            op("pool", lambda e, c=c: e.dma_start(
                out=self.GTOK[:, (c - 16) * P:(c - 15) * P].rearrange("(n p) j -> p n j", p=P), in_=tokst),
               reads=[tokb], writes=[self.gtok_buf], dma=True)
    self.fence()
    rawba, bgf = xflat[:, 0:T], xflat[:, T:2 * T]
    tmp2 = mflat[:, 0:T]
    bgt = mflat[:, T:2 * T].rearrange("p (n f) -> p n f", f=P)
    op("sp", lambda e: e.dma_start(out=rawba, in_=self.RAWBA[:, :]), reads=[self.raw_buf], writes=[self.xtb], dma=True)
    op("act", lambda e: e.activation(out=bgf, in_=rawba, func=AF.Sigmoid), reads=[self.xtb], writes=[self.xtb])
    op("act", lambda e: e.activation(out=tmp2, in_=rawba, func=AF.Exp, bias=self.gsm[:, 1:2], scale=1.0),
       reads=[self.xtb, self.gcb], writes=[self.mtb])
    op("act", lambda e: e.activation(out=tmp2, in_=tmp2, func=AF.Ln, bias=1.0, scale=1.0),
       reads=[self.mtb], writes=[self.mtb])
    op("dve", lambda e: e.tensor_scalar(out=tmp2, in0=tmp2, scalar1=self.gsm[:, 0:1], scalar2=None, op0=ALU.mult),
       reads=[self.mtb, self.gcb], writes=[self.mtb])
    for r0 in (32, 96):
        op("sp", lambda e, r0=r0: e.dma_start(out=bgf[r0:r0 + 32, :], in_=tmp2[r0:r0 + 32, :]),
           reads=[self.mtb], writes=[self.xtb], dma=True)
    for n4 in range(0, NCH, 4):
        pid = 2 + (n4 // 4) % 2
        ps_, psb_ = self.ps[pid], self.psb[pid]
        for q in range(4):
            op("pe", lambda e, ps_=ps_, q=q, n4=n4: e.transpose(
                ps_[:, q * P:(q + 1) * P], bgf[:, (n4 + q) * P:(n4 + q + 1) * P], self.ident_f[:]),
               reads=[self.xtb, self.cb], writes=[psb_])
        self.copy_evac(self.evac_engine(), bgt[:, n4:n4 + 4, :], ps_[:, :].rearrange("p (n f) -> p n f", f=P),
                       [psb_], [self.mtb])
    W = NCH * 64

    def tab(region, k):
        return region[:, k * W:(k + 1) * W].rearrange("p (n f) -> p n f", f=64)

    BETA, GC, EGC, BEGC = tab(xflat, 0), tab(xflat, 1), tab(xflat, 2), tab(xflat, 3)
    EGE, EGT = tab(mflat, 0), tab(mflat, 1)
    tb = self.rstdb
    for n in range(NCH):
        ps_, psb_ = self.ps[4 + n % 2], self.psb[4 + n % 2]
        gf, gb = bgt[:, n, 32:64], bgt[:, n, 96:128]
        mm = [(1, gf, 0), (3, gb, 32), (0, gf, 64), (2, gb, 96), (4, gf, 128), (4, gb, 160)]
        for (ti, rhs, c0) in mm:
            op("pe", lambda e, ps_=ps_, ti=ti, rhs=rhs, c0=c0: e.matmul(
                ps_[:, c0:c0 + 32], lhsT=self.tri[:, ti, :], rhs=rhs, start=True, stop=True),
               reads=[self.mtb, self.gcb], writes=[psb_])
        op("dve", lambda e, ps_=ps_, n=n: e.tensor_copy(out=GC[:, n, :], in_=ps_[:, 0:64]),
           reads=[psb_, self.mtb], writes=[tb, self.xtb])
        op("act", lambda e, ps_=ps_, n=n: e.activation(out=EGC[:, n, :], in_=ps_[:, 0:64], func=AF.Exp),
           reads=[psb_], writes=[tb, self.xtb])
        op("dve", lambda e, n=n: e.tensor_copy(
            out=BETA[:, n, :].rearrange("p (d h) -> p d h", d=2),
            in_=bgt[:, n, :].rearrange("p (d j h) -> p d j h", d=2, j=2)[:, :, 0, :]),
           reads=[self.mtb], writes=[tb, self.xtb])
        op("dve", lambda e, n=n: e.tensor_tensor(out=BEGC[:, n, :], in0=BETA[:, n, :], in1=EGC[:, n, :], op=ALU.mult),
           reads=[tb], writes=[tb, self.xtb])
    for n in range(NCH):
        ps_, psb_ = self.ps[4 + n % 2], self.psb[4 + n % 2]
        gf, gb = bgt[:, n, 32:64], bgt[:, n, 96:128]
        mm = [(0, gf, 64), (2, gb, 96), (4, gf, 128), (4, gb, 160)]
        for (ti, rhs, c0) in mm:
            op("pe", lambda e, ps_=ps_, ti=ti, rhs=rhs, c0=c0: e.matmul(
                ps_[:, c0:c0 + 32], lhsT=self.tri[:, ti, :], rhs=rhs, start=True, stop=True),
               reads=[self.mtb, self.gcb], writes=[psb_])
        op("act", lambda e, ps_=ps_, n=n: e.activation(out=EGE[:, n, :], in_=ps_[:, 64:128], func=AF.Exp),
           reads=[psb_], writes=[self.mtb])
        op("act", lambda e, ps_=ps_, n=n: e.activation(out=EGT[:, n, :], in_=ps_[:, 128:192], func=AF.Exp),
           reads=[psb_], writes=[self.mtb])
    import os
    stop = int(os.environ.get("GDN_STOP", "9"))
    if stop <= 3:
        self.phase_post(layer, self.GOT, self.got_buf, 32, self.wb_gdn_out[j], slot, last)
        return
    qT, kT = aflat[:, 0:T], aflat[:, T:2 * T]
    ktok = aflat[:, 2 * T:3 * T].rearrange("p (n j) -> p n j", j=P)
    vtok = aflat[:, 3 * T:4 * T].rearrange("p (n j) -> p n j", j=P)
    zT = aflat[:, 4 * T:5 * T]
    hf32 = hflat.bitcast(F32)
    oacc = hf32[:, 0:T].rearrange("p (n j) -> p n j", j=P)
    oab = self.htb
    base = [T]

    def sc_f32():
        a = hf32[:, base[0]:base[0] + P]
        base[0] += P
        return a, Buf("g4")

    def sc_b16():
        a = hf32[:, base[0]:base[0] + P // 2].bitcast(BF16)
        base[0] += P // 2
        return a, Buf("g4")

    dg, dgb = sc_f32()
    Dp, Dpb = sc_f32()
    Dn, Dnb = sc_f32()
    E1, E1b = sc_f32()
    E2, E2b = sc_f32()
    u_, ub = sc_f32()
    S_, Sb = sc_f32()
    P2s, P2b = sc_f32()
    tmo, tmob = sc_f32()
    M_, Mb = sc_b16()
    Nn = [sc_b16() for _ in range(2)]
    NT = [sc_b16() for _ in range(2)]
    TTt = [sc_b16() for _ in range(2)]
    vb_, vbb = sc_b16()
    kbg, kbgb = sc_b16()
    kte, kteb = sc_b16()
    wT, wTb = sc_b16()
    vnew, vnb = sc_b16()
    QKT, QKTb = sc_b16()
    Sbf, Sbfb = sc_b16()
    onb_, onbb = sc_b16()

    def sc_b16x2():
        a_ = hf32[:, base[0]:base[0] + P].bitcast(BF16)
        base[0] += P
        return a_, Buf("g4")

    TT2 = [sc_b16x2() for _ in range(2)]
    YY, YYb = sc_b16x2()
    outT = mflat[:, T:2 * T].bitcast(BF16)[:, 0:T]
    outTb = Buf("outT")
    stt_ = self.tmpf[1]
    sttb = self.tmpfb[1]
    A0, A0b = self.ps[0], self.psb[0]
    A1, A1b = self.ps[1], self.psb[1]
    A2, A2b = self.ps[2], self.psb[2]
    A3, A3b = self.ps[3], self.psb[3]
    A6, A6b = self.ps[6], self.psb[6]
    A0h = A0[:, :].bitcast(BF16)
    g4bufs = [dgb, Dpb, Dnb, E1b, E2b, ub, Sb, P2b, tmob, Mb, vbb, kbgb, kteb, wTb, vnb, QKTb, Sbfb, onbb, outTb,
              Nn[0][1], Nn[1][1], NT[0][1], NT[1][1], TTt[0][1], TTt[1][1], TT2[0][1], TT2[1][1], YYb]
    self.fence(g4bufs)
    nhv = int(os.environ.get("GDN_NHV", "32"))
    stage_ = int(os.environ.get("GDN_STAGE", "9"))
    for hv in range(nhv):
        hk = hv // 2
        op("sp", lambda e, hk=hk: e.dma_start(out=qT, in_=self.GQT[hk]), reads=[self.gqt_buf], writes=[self.arb], dma=True)
        op("sp", lambda e, hk=hk: e.dma_start(out=kT, in_=self.GQT[16 + hk]), reads=[self.gqt_buf], writes=[self.arb], dma=True)
        op("sp", lambda e, hk=hk: e.dma_start(
            out=ktok, in_=self.GTOK[:, hk * P:(hk + 1) * P].rearrange("(n p) j -> p n j", p=P)),
           reads=[self.gtok_buf], writes=[self.arb], dma=True)
        op("sp", lambda e, hv=hv: e.dma_start(
            out=vtok, in_=self.GTOK[:, (16 + hv) * P:(17 + hv) * P].rearrange("(n p) j -> p n j", p=P)),
           reads=[self.gtok_buf], writes=[self.arb], dma=True)
        op("sp", lambda e, hv=hv: e.dma_start(out=zT, in_=self.RAW[64 + hv]), reads=[self.raw_buf], writes=[self.arb], dma=True)
        op("act", lambda e: e.activation(out=zT, in_=zT, func=AF.Silu), reads=[self.arb], writes=[self.arb])
        for dd in range(2):
            col = dd * 32 + hv
            op("dve", lambda e: e.memset(S_, 0.0), writes=[Sb])
            op("dve", lambda e: e.memset(Sbf, 0.0), writes=[Sbfb])
            chunks = range(NCH) if dd == 0 else range(NCH - 1, -1, -1)
            for n in chunks:
                cs = slice(n * P, (n + 1) * P)
                op("pe", lambda e, cs=cs: e.matmul(A0[:, 0:P], lhsT=kT[:, cs], rhs=kT[:, cs], start=True, stop=True),
                   reads=[self.arb], writes=[A0b])
                op("pe", lambda e, cs=cs: e.matmul(A0[:, P:2 * P], lhsT=kT[:, cs], rhs=qT[:, cs], start=True, stop=True),
                   reads=[self.arb], writes=[A0b])
                op("dve", lambda e, n=n, col=col: e.tensor_scalar(out=dg, in0=self.ident_f[:, :], scalar1=GC[:, n, col:col + 1],
                                                                  scalar2=None, op0=ALU.mult),
                   reads=[tb, self.cb], writes=[dgb])
                op("pe", lambda e: e.matmul(A0[:, 2 * P:3 * P], lhsT=self.tri[:, 4, :], rhs=dg, start=True, stop=True),
                   reads=[dgb, self.gcb], writes=[A0b])
                op("dve", lambda e, n=n, col=col: e.tensor_scalar(out=Dp, in0=A0[:, 2 * P:3 * P], scalar1=GC[:, n, col:col + 1],
                                                                  scalar2=0.0, op0=ALU.subtract, op1=ALU.max),
                   reads=[A0b, tb], writes=[Dpb])
                op("dve", lambda e, n=n, col=col: e.tensor_scalar(out=Dn, in0=A0[:, 2 * P:3 * P], scalar1=GC[:, n, col:col + 1],
                                                                  scalar2=0.0, op0=ALU.subtract, op1=ALU.min),
                   reads=[A0b, tb], writes=[Dnb])
                op("act", lambda e: e.activation(out=E1, in_=Dp, func=AF.Exp, scale=-1.0), reads=[Dpb], writes=[E1b])
                op("act", lambda e: e.activation(out=E2, in_=Dn, func=AF.Exp), reads=[Dnb], writes=[E2b])
                op("pool", lambda e, dd=dd: e.tensor_tensor(out=E1, in0=E1, in1=self.ntri[:, dd, :], op=ALU.mult),
                   reads=[E1b, self.gcb], writes=[E1b])
                op("pool", lambda e, dd=dd: e.tensor_tensor(out=E2, in0=E2, in1=self.tri[:, 1 if dd == 0 else 3, :], op=ALU.mult),
                   reads=[E2b, self.gcb], writes=[E2b])
                op("dve", lambda e, n=n, col=col: e.scalar_tensor_tensor(
                    out=M_, in0=A0[:, 0:P], scalar=BETA[:, n, col:col + 1], in1=E1, op0=ALU.mult, op1=ALU.mult),
                   reads=[A0b, tb, E1b], writes=[Mb])
                op("dve", lambda e: e.tensor_tensor(out=QKT, in0=A0[:, P:2 * P], in1=E2, op=ALU.mult),
                   reads=[A0b, E2b], writes=[QKTb])
                if stage_ <= 1:
                    continue
                op("pe", lambda e: e.transpose(A1[:, :].bitcast(BF16)[:, 0:P], M_, self.ident_b[:]),
                   reads=[Mb, self.cb], writes=[A1b])
                A1h = A1[:, :].bitcast(BF16)
                NT0, NT0b = NT[0]
                op("dve", lambda e: e.tensor_copy(out=NT0, in_=A1h[:, 0:P]), reads=[A1b], writes=[NT0b])
                ma = (lambda l_: l_) if dd == 0 else (lambda l_: 7 + l_)
                mb_ = (lambda l_: 7 + l_) if dd == 0 else (lambda l_: l_)
                curTT, curTTb = None, None
                for l_ in range(7):
                    Ml, Mlb = Nn[0]
                    Ul, Ulb = Nn[1]
                    ia_, ib_ = ma(l_), mb_(l_)
                    op("pool", lambda e, ia_=ia_, Ml=Ml: e.tensor_tensor(out=Ml, in0=M_, in1=self.lvm[:, ia_, :], op=ALU.mult),
                       reads=[Mb, self.gcb], writes=[Mlb])
                    op("pool", lambda e, ib_=ib_, Ul=Ul: e.tensor_tensor(out=Ul, in0=NT0, in1=self.lvm[:, ib_, :], op=ALU.mult),
                       reads=[NT0b, self.gcb], writes=[Ulb])
                    nTT, nTTb = TT2[l_ % 2]
                    if l_ == 0:
                        op("dve", lambda e, nTT=nTT, Ml=Ml: e.tensor_tensor(out=nTT[:, 0:P], in0=Ml, in1=self.ident_b[:, :], op=ALU.add),
                           reads=[Mlb, self.cb], writes=[nTTb])
                        op("dve", lambda e, nTT=nTT, Ul=Ul: e.tensor_tensor(out=nTT[:, P:2 * P], in0=Ul, in1=self.ident_b[:, :], op=ALU.add),
                           reads=[Ulb, self.cb], writes=[nTTb])
                    else:
                        cT, cTT = curTT[:, 0:P], curTT[:, P:2 * P]
                        op("pe", lambda e, Ul=Ul, cT=cT: e.matmul(A2[:, 0:P], lhsT=Ul, rhs=cT, start=True, stop=True),
                           reads=[Ulb, curTTb], writes=[A2b])
                        op("pe", lambda e, Ul=Ul, cT=cT: e.matmul(A2[:, P:2 * P], lhsT=cT, rhs=Ul, start=True, stop=True),
                           reads=[Ulb, curTTb], writes=[A2b])
                        op("dve", lambda e: e.tensor_copy(out=YY, in_=A2[:, 0:2 * P]), reads=[A2b], writes=[YYb])
                        op("pe", lambda e, cT=cT: e.matmul(A3[:, 0:P], lhsT=self.ident_b[:, :], rhs=cT, start=True, stop=False),
                           reads=[curTTb, self.cb], writes=[A3b])
                        op("pe", lambda e, cTT=cTT: e.matmul(A3[:, 0:P], lhsT=cTT, rhs=YY[:, 0:P], start=False, stop=True),
                           reads=[curTTb, YYb], writes=[A3b])
                        op("pe", lambda e, cTT=cTT: e.matmul(A3[:, P:2 * P], lhsT=self.ident_b[:, :], rhs=cTT, start=True, stop=False),
                           reads=[curTTb, self.cb], writes=[A3b])
                        op("pe", lambda e, cTT=cTT: e.matmul(A3[:, P:2 * P], lhsT=YY[:, 0:P], rhs=cTT, start=False, stop=True),
                           reads=[curTTb, YYb], writes=[A3b])
                        op("dve", lambda e, nTT=nTT: e.tensor_copy(out=nTT, in_=A3[:, 0:2 * P]), reads=[A3b], writes=[nTTb])
                    curTT, curTTb = nTT, nTTb
                curT, curTb = curTT[:, P:2 * P], curTTb
                if stage_ <= 2:
                    continue
                op("dve", lambda e, n=n, col=col: e.tensor_scalar(out=vb_, in0=vtok[:, n, :], scalar1=BETA[:, n, col:col + 1],
                                                                   scalar2=None, op0=ALU.mult),
                   reads=[self.arb, tb], writes=[vbb])
                op("dve", lambda e, n=n, col=col: e.tensor_scalar(out=kbg, in0=ktok[:, n, :], scalar1=BEGC[:, n, col:col + 1],
                                                                   scalar2=None, op0=ALU.mult),
                   reads=[self.arb, tb], writes=[kbgb])
                op("dve", lambda e, n=n, col=col: e.tensor_scalar(out=kte, in0=ktok[:, n, :], scalar1=EGE[:, n, col:col + 1],
                                                                   scalar2=None, op0=ALU.mult),
                   reads=[self.arb, self.mtb], writes=[kteb])
                op("pe", lambda e, curT=curT: e.matmul(A6[:, 0:P], lhsT=curT, rhs=vb_, start=True, stop=True),
                   reads=[curTb, vbb], writes=[A6b])
                op("pe", lambda e, curT=curT: e.matmul(A6[:, P:2 * P], lhsT=kbg, rhs=curT, start=True, stop=True),
                   reads=[curTb, kbgb], writes=[A6b])
                op("dve", lambda e: e.tensor_copy(out=u_, in_=A6[:, 0:P]), reads=[A6b], writes=[ub])
                op("dve", lambda e: e.tensor_copy(out=wT, in_=A6[:, P:2 * P]), reads=[A6b], writes=[wTb])
                if stage_ <= 3:
                    continue
                op("pe", lambda e: e.matmul(A6[:, 2 * P:3 * P], lhsT=wT, rhs=Sbf, start=True, stop=True),
                   reads=[wTb, Sbfb], writes=[A6b])
                op("dve", lambda e: e.tensor_tensor(out=vnew, in0=u_, in1=A6[:, 2 * P:3 * P], op=ALU.subtract),
                   reads=[ub, A6b], writes=[vnb])
                op("pe", lambda e, cs=cs: e.matmul(A3[:, P:2 * P], lhsT=qT[:, cs], rhs=Sbf, start=True, stop=True),
                   reads=[self.arb, Sbfb], writes=[A3b])
                op("pe", lambda e: e.matmul(A3[:, 2 * P:3 * P], lhsT=QKT, rhs=vnew, start=True, stop=True),
                   reads=[QKTb, vnb], writes=[A3b])
                op("pe", lambda e: e.matmul(A3[:, 3 * P:4 * P], lhsT=kte, rhs=vnew, start=True, stop=True),
                   reads=[kteb, vnb], writes=[A3b])
                op("act", lambda e: e.activation(out=P2s, in_=A3[:, 2 * P:3 * P], func=AF.Copy), reads=[A3b], writes=[P2b])
                if dd == 0:
                    op("dve", lambda e, n=n, col=col: e.scalar_tensor_tensor(
                        out=oacc[:, n, :], in0=A3[:, P:2 * P], scalar=EGC[:, n, col:col + 1], in1=P2s,
                        op0=ALU.mult, op1=ALU.add), reads=[A3b, tb, P2b], writes=[oab])
                else:
                    op("dve", lambda e, n=n, col=col: e.scalar_tensor_tensor(
                        out=tmo, in0=A3[:, P:2 * P], scalar=EGC[:, n, col:col + 1], in1=P2s,
                        op0=ALU.mult, op1=ALU.add), reads=[A3b, tb, P2b], writes=[tmob])
                    op("pool", lambda e, n=n: e.tensor_tensor(out=oacc[:, n, :], in0=oacc[:, n, :], in1=tmo, op=ALU.add),
                       reads=[tmob, oab], writes=[oab])
                op("dve", lambda e, n=n, col=col: e.scalar_tensor_tensor(
                    out=S_, in0=S_, scalar=EGT[:, n, col:col + 1], in1=A3[:, 3 * P:4 * P], op0=ALU.mult, op1=ALU.add),
                   reads=[Sb, self.mtb, A3b], writes=[Sb])
                op("pool", lambda e: e.tensor_copy(out=Sbf, in_=S_), reads=[Sb], writes=[Sbfb])
        for n in range(NCH):
            op("act", lambda e, n=n: e.activation(out=tmo, in_=oacc[:, n, :], func=AF.Square, accum_out=stt_[:, n:n + 1]),
               reads=[oab], writes=[tmob, sttb])
        op("dve", lambda e: e.tensor_scalar(out=stt_[:, 0:NCH], in0=stt_[:, 0:NCH], scalar1=1.0 / P, scalar2=RMS_EPS,
                                            op0=ALU.mult, op1=ALU.add), reads=[sttb], writes=[sttb])
        op("act", lambda e: e.activation(out=stt_[:, 0:NCH], in_=stt_[:, 0:NCH], func=AF.Sqrt), reads=[sttb], writes=[sttb])
        op("dve", lambda e: e.reciprocal(out=stt_[:, 0:NCH], in_=stt_[:, 0:NCH]), reads=[sttb], writes=[sttb])
        for n in range(NCH):
            op("dve", lambda e, n=n: e.tensor_scalar(out=onb_, in0=oacc[:, n, :], scalar1=stt_[:, n:n + 1], scalar2=None,
                                                     op0=ALU.mult), reads=[oab, sttb], writes=[onbb])
            op("pe", lambda e: e.transpose(A1[:, :].bitcast(BF16)[:, P:2 * P], onb_, self.ident_b[:]),
               reads=[onbb, self.cb], writes=[A1b])
            op("dve", lambda e, n=n: e.scalar_tensor_tensor(
                out=outT[:, n * P:(n + 1) * P], in0=A1[:, :].bitcast(BF16)[:, P:2 * P], scalar=self.gsm[:, 2:3],
                in1=zT[:, n * P:(n + 1) * P], op0=ALU.mult, op1=ALU.mult),
               reads=[A1b, self.gcb, self.arb], writes=[outTb])
        op("pool", lambda e, hv=hv: e.dma_start(out=self.GOT[hv], in_=outT), reads=[outTb], writes=[self.got_buf], dma=True)
    self.fence(g4bufs)
    self.phase_post(layer, self.GOT, self.got_buf, 32, self.wb_gdn_out[j], slot, last)


Prog.gdn_setup = _gdn_setup
Prog.gdn_convert = _gdn_convert
Prog.gdn_layer = _gdn_layer


SEQ_T = 4096
NSLOT = 1
DEPTH = 4


def build_program():
    layers = [(i % 3, i // 3) for i in range(DEPTH)]
    n_gdn, n_swa, n_fnet = (DEPTH + 2) // 3, (DEPTH + 1) // 3, DEPTH // 3
    p = Prog(SEQ_T, NSLOT, layers, n_gdn, n_swa, n_fnet, DEPTH)
    p.gdn_setup()
    p.swa_setup()
    p.fnet_setup()
    p.init_consts()
    p.phase_convert_ffn()
    p.gdn_convert()
    p.swa_convert()
    for j in range(n_fnet):
        p.convert_weight(p.w_fnet[j], p.wb_fnet[j], D_MODEL, D_MODEL)
    p.conv_fence()
    for slot in range(NSLOT):
        p.phase_ingest(slot)
        for li, (kind, j) in enumerate(layers):
            last = li == DEPTH - 1
            p.fence()
            if kind == 0:
                p.gdn_layer(li, j, slot, last)
            elif kind == 1:
                p.swa_layer(li, j, slot, last)
            else:
                p.fnet_layer(li, j, slot, last)
            p.fence()
    p.finish()
    return p


def kernel(**inputs):
    x_prompt = np.asarray(inputs["x_prompt"], np.float32)
    x_sample = np.asarray(inputs["x_sample"], np.float32)
    p = build_program()
    consts = {}
    consts.update(common_consts())
    consts.update(fnet_consts(SEQ_T))
    consts.update(swa_consts())
    consts.update(gdn_consts())
    shared = {k: np.ascontiguousarray(np.asarray(v, np.float32)) for k, v in inputs.items()
              if k not in ("x_prompt", "x_sample")}

    def run(xs):
        in_maps = []
        for c in range(xs.shape[0]):
            m = {"x_in": np.ascontiguousarray(xs[c:c + 1])}
            m.update(shared)
            m.update(consts)
            in_maps.append(m)
        res = run_bass_kernel_spmd(p.nc, in_maps, core_ids=list(range(xs.shape[0])))
        return np.stack([res.results[c]["y_out"][0] for c in range(xs.shape[0])], 0).astype(np.float32)

    y_sample = run(x_sample)
    y_prompt = run(x_prompt)
    return (y_prompt, y_sample)
```
